# Optimizing a Trainium2 kernel written in Bass

```python
import math
import jax
import jax.numpy as jnp
from jax import lax
import numpy as np

D_MODEL = 1024
BATCH = 8
SEQ = 4096
DEPTH = 4

N_EVEN = (DEPTH + 1) // 2
N_ODD = DEPTH // 2
HEAD_DIM = 64
CONV_W = 3
A_W = D_MODEL // 2
B_W = D_MODEL - A_W
POOL_WINDOWS = (2, 4, 8, 16)
N_POOL = len(POOL_WINDOWS)
POOL_G = B_W // N_POOL
EV_IN_W = 3 * A_W + B_W
MIX_W = A_W + B_W
H_FOX = D_MODEL // (2 * HEAD_DIM)
H_MOBA = D_MODEL // (2 * HEAD_DIM)
H_ATT = H_FOX + H_MOBA
ATT_W = H_ATT * HEAD_DIM
OD_IN_W = 3 * ATT_W + H_FOX
ATTN_SCALE = HEAD_DIM ** -0.5
FOX_Q_BLOCK = 128
MOBA_BLOCK = 256
MOBA_TOPK = 3
MOBA_Q_CHUNK = 32
T5_BUCKETS = 32
T5_MAX_DIST = 128
FORGET_BIAS_INIT = 2.0
D_FF = 2816
RMS_EPS = 1e-6

kernel_name = "hybrid_conv_pool_fox_moba_trunk"


def rmsnorm(x, g):
    xf = x.astype(jnp.float32)
    y = xf * lax.rsqrt(jnp.mean(xf * xf, axis=-1, keepdims=True) + RMS_EPS)
    return (y * g.astype(jnp.float32)).astype(x.dtype)


def causal_dwconv(u, w):
    s = u.shape[1]
    up = jnp.pad(u, ((0, 0), (CONV_W - 1, 0), (0, 0)))
    return sum(w[i] * up[:, i:i + s] for i in range(CONV_W))


def multiscale_pool(u, pool_w, pool_scale):
    bsz, s, _ = u.shape
    cs = jnp.cumsum(jnp.pad(u.astype(jnp.float32), ((0, 0), (1, 0), (0, 0))), axis=1)
    pos = jnp.arange(s)
    groups = []
    for g, w in enumerate(POOL_WINDOWS):
        csg = cs[..., g * POOL_G:(g + 1) * POOL_G]
        hi = csg[:, 1:]
        lo = jnp.pad(csg[:, :s + 1 - w], ((0, 0), (w - 1, 0), (0, 0)))
        cnt = jnp.minimum(pos + 1, w).astype(jnp.float32)[None, :, None]
        groups.append((hi - lo) / cnt - u[..., g * POOL_G:(g + 1) * POOL_G].astype(jnp.float32))
    p = jnp.stack(groups, axis=2).astype(u.dtype)
    y = jnp.einsum('bsgc,gcd->bsgd', p, pool_w).reshape(bsz, s, B_W)
    return y * pool_scale


def even_mixer(h, w_in, conv_w, pool_w, pool_scale, w_out):
    z = h @ w_in
    gate_b = z[..., :A_W]
    gate_c = z[..., A_W:2 * A_W]
    val = z[..., 2 * A_W:3 * A_W]
    pool_in = z[..., 3 * A_W:]
    y_a = gate_b * causal_dwconv(gate_c * val, conv_w)
    y_b = multiscale_pool(pool_in, pool_w, pool_scale)
    return jnp.concatenate([y_a, y_b], axis=-1) @ w_out


def t5_bucket(dist):
    dist = jnp.maximum(dist, 0)
    exact = T5_BUCKETS // 2
    d_f = jnp.maximum(dist, 1).astype(jnp.float32)
    log_b = exact + (jnp.log(d_f / exact) / math.log(T5_MAX_DIST / exact)
                     * (T5_BUCKETS - exact)).astype(jnp.int32)
    log_b = jnp.minimum(log_b, T5_BUCKETS - 1)
    return jnp.where(dist < exact, dist, log_b)


def forgetting_attention(q, k, v, log_f):
    bsz, nh, s, dh = q.shape
    fcum = jnp.cumsum(log_f, axis=-1)
    nq = s // FOX_Q_BLOCK
    q_blocks = q.reshape(bsz, nh, nq, FOX_Q_BLOCK, dh).transpose(2, 0, 1, 3, 4)
    f_blocks = fcum.reshape(bsz, nh, nq, FOX_Q_BLOCK).transpose(2, 0, 1, 3)
    kpos = jnp.arange(s)

    def one_block(args):
        i, q_i, f_i = args
        qpos = i * FOX_Q_BLOCK + jnp.arange(FOX_Q_BLOCK)
        logits = jnp.einsum('bhqd,bhkd->bhqk', q_i, k).astype(jnp.float32) * ATTN_SCALE
        logits = logits + f_i[..., :, None] - fcum[..., None, :]
        logits = jnp.where(kpos[None, :] <= qpos[:, None], logits, -jnp.inf)
        p = jax.nn.softmax(logits, axis=-1)
        return jnp.einsum('bhqk,bhkd->bhqd', p.astype(v.dtype), v)

    out = lax.map(one_block, (jnp.arange(nq), q_blocks, f_blocks))
    return out.transpose(1, 2, 0, 3, 4).reshape(bsz, nh, s, dh)


def moba_attention(q, k, v, rel_bias):
    bsz, nh, s, dh = q.shape
    n_blk = -(-s // MOBA_BLOCK)
    pad = n_blk * MOBA_BLOCK - s
    k_p = jnp.pad(k, ((0, 0), (0, 0), (0, pad), (0, 0)))
    v_p = jnp.pad(v, ((0, 0), (0, 0), (0, pad), (0, 0)))
    k_blk = k_p.reshape(bsz, nh, n_blk, MOBA_BLOCK, dh)
    v_blk = v_p.reshape(bsz, nh, n_blk, MOBA_BLOCK, dh)
    k_mean = jnp.mean(k_blk.astype(jnp.float32), axis=3)
    topk = min(MOBA_TOPK, n_blk)
    nq = s // MOBA_Q_CHUNK
    q_chunks = q.reshape(bsz, nh, nq, MOBA_Q_CHUNK, dh).transpose(2, 0, 1, 3, 4)
    blk_ids = jnp.arange(n_blk)
    kpos_in_blk = jnp.arange(MOBA_BLOCK)
    head_ids = jnp.arange(nh)[:, None, None, None]
    bias_t = rel_bias.T
    gather_blocks = jax.vmap(jax.vmap(lambda kb, ix: kb[ix]))

    def one_chunk(args):
        c, q_i = args
        start = c * MOBA_Q_CHUNK
        cur = start // MOBA_BLOCK
        qpos = start + jnp.arange(MOBA_Q_CHUNK)
        gate = jnp.einsum('bhqd,bhnd->bhqn', q_i.astype(jnp.float32), k_mean)
        gate = jnp.where(blk_ids < cur, gate, -jnp.inf)
        _, idx = lax.top_k(gate, topk)
        valid = idx < cur
        k_sel = gather_blocks(k_blk, idx)
        v_sel = gather_blocks(v_blk, idx)
        kpos_sel = idx[..., None] * MOBA_BLOCK + kpos_in_blk
        bias_sel = bias_t[head_ids, t5_bucket(qpos[:, None, None] - kpos_sel)]
        l_sel = jnp.einsum('bhqd,bhqtld->bhqtl', q_i, k_sel).astype(jnp.float32) * ATTN_SCALE
        l_sel = jnp.where(valid[..., None], l_sel + bias_sel, -jnp.inf)
        k_own = lax.dynamic_slice_in_dim(k_p, cur * MOBA_BLOCK, MOBA_BLOCK, axis=2)
        v_own = lax.dynamic_slice_in_dim(v_p, cur * MOBA_BLOCK, MOBA_BLOCK, axis=2)
        kpos_own = cur * MOBA_BLOCK + kpos_in_blk
        dist_own = qpos[:, None] - kpos_own[None, :]
        bias_own = rel_bias[t5_bucket(dist_own)].transpose(2, 0, 1)
        l_own = jnp.einsum('bhqd,bhld->bhql', q_i, k_own).astype(jnp.float32) * ATTN_SCALE
        l_own = jnp.where(dist_own >= 0, l_own + bias_own, -jnp.inf)
        logits = jnp.concatenate(
            [l_sel.reshape(bsz, nh, MOBA_Q_CHUNK, topk * MOBA_BLOCK), l_own], axis=-1)
        p = jax.nn.softmax(logits, axis=-1).astype(v.dtype)
        p_sel = p[..., :topk * MOBA_BLOCK].reshape(bsz, nh, MOBA_Q_CHUNK, topk, MOBA_BLOCK)
        p_own = p[..., topk * MOBA_BLOCK:]
        return (jnp.einsum('bhqtl,bhqtld->bhqd', p_sel, v_sel)
                + jnp.einsum('bhql,bhld->bhqd', p_own, v_own))

    out = lax.map(one_chunk, (jnp.arange(nq), q_chunks))
    return out.transpose(1, 2, 0, 3, 4).reshape(bsz, nh, s, dh)


def odd_mixer(h, w_in, b_f, w_out, rel_bias):
    bsz, s, _ = h.shape
    z = h @ w_in
    qkv = z[..., :3 * ATT_W].reshape(bsz, s, 3, H_ATT, HEAD_DIM).transpose(2, 0, 3, 1, 4)
    q, k, v = qkv[0], qkv[1], qkv[2]
    log_f = jax.nn.log_sigmoid((z[..., 3 * ATT_W:] + b_f).astype(jnp.float32)).transpose(0, 2, 1)
    y_c = forgetting_attention(q[:, :H_FOX], k[:, :H_FOX], v[:, :H_FOX], log_f)
    y_d = moba_attention(q[:, H_FOX:], k[:, H_FOX:], v[:, H_FOX:], rel_bias)
    y = jnp.concatenate([y_c, y_d], axis=1).transpose(0, 2, 1, 3).reshape(bsz, s, ATT_W)
    return y @ w_out


def conv_ffn(h, w_in, conv_w, conv_b, w_out):
    z = h @ w_in
    u, g = z[..., :D_FF], z[..., D_FF:]
    a = causal_dwconv(u, conv_w) + conv_b
    return (jax.nn.silu(a) * g) @ w_out


def setup_inputs(seed: int = 0) -> dict:
    key = jax.random.key(seed)
    ks = jax.random.split(key, 17)
    f32 = jnp.float32

    def nrm(k, shape, scale):
        return jax.random.normal(k, shape, f32) * scale

    out_scale = (2 * DEPTH) ** -0.5
    return {
        "x": nrm(ks[0], (BATCH, SEQ, D_MODEL), 1.0),
        "mix_norm_g": 1.0 + nrm(ks[1], (DEPTH, D_MODEL), 0.05),
        "ffn_norm_g": 1.0 + nrm(ks[2], (DEPTH, D_MODEL), 0.05),
        "final_norm_g": 1.0 + nrm(ks[3], (D_MODEL,), 0.05),
        "ev_w_in": nrm(ks[4], (N_EVEN, D_MODEL, EV_IN_W), D_MODEL ** -0.5),
        "ev_conv_w": nrm(ks[5], (N_EVEN, CONV_W, A_W), CONV_W ** -0.5),
        "ev_pool_w": nrm(ks[6], (N_EVEN, N_POOL, POOL_G, POOL_G), POOL_G ** -0.5),
        "ev_pool_scale": 1.0 + nrm(ks[7], (N_EVEN, B_W), 0.1),
        "ev_w_out": nrm(ks[8], (N_EVEN, MIX_W, D_MODEL), MIX_W ** -0.5 * out_scale),
        "od_w_in": nrm(ks[9], (N_ODD, D_MODEL, OD_IN_W), D_MODEL ** -0.5),
        "od_b_f": FORGET_BIAS_INIT + nrm(ks[10], (N_ODD, H_FOX), 0.5),
        "od_w_out": nrm(ks[11], (N_ODD, ATT_W, D_MODEL), ATT_W ** -0.5 * out_scale),
        "rel_bias": nrm(ks[12], (T5_BUCKETS, H_MOBA), 0.5),
        "ffn_w_in": nrm(ks[13], (DEPTH, D_MODEL, 2 * D_FF), D_MODEL ** -0.5),
        "ffn_conv_w": nrm(ks[14], (DEPTH, CONV_W, D_FF), CONV_W ** -0.5),
        "ffn_conv_b": nrm(ks[15], (DEPTH, D_FF), 0.02),
        "ffn_w_out": nrm(ks[16], (DEPTH, D_FF, D_MODEL), D_FF ** -0.5 * out_scale),
    }


def reference(x, mix_norm_g, ffn_norm_g, final_norm_g, ev_w_in, ev_conv_w, ev_pool_w,
              ev_pool_scale, ev_w_out, od_w_in, od_b_f, od_w_out, rel_bias,
              ffn_w_in, ffn_conv_w, ffn_conv_b, ffn_w_out):
    h = x
    for layer in range(DEPTH):
        hn = rmsnorm(h, mix_norm_g[layer])
        if layer % 2 == 0:
            e = layer // 2
            h = h + even_mixer(hn, ev_w_in[e], ev_conv_w[e], ev_pool_w[e],
                               ev_pool_scale[e], ev_w_out[e])
        else:
            o = layer // 2
            h = h + odd_mixer(hn, od_w_in[o], od_b_f[o], od_w_out[o], rel_bias)
        h = h + conv_ffn(rmsnorm(h, ffn_norm_g[layer]), ffn_w_in[layer],
                         ffn_conv_w[layer], ffn_conv_b[layer], ffn_w_out[layer])
    return rmsnorm(h, final_norm_g)
```

```python
import contextlib
import os
import numpy as np
import concourse.bass as bass
import concourse.mybir as mybir
from concourse.bass_utils import run_bass_kernel_spmd

F32 = mybir.dt.float32
BF16 = mybir.dt.bfloat16
AF = mybir.ActivationFunctionType
ALU = mybir.AluOpType
AX = mybir.AxisListType

S = 4096
D = 1024
DFF = 2816
TT = 256
NT = S // TT
NEG = -30000.0
POOL_W = (2, 4, 8, 16)
FFN_GROUPS = ((0, 8), (8, 7), (15, 7))
GCOLS = 25088

SM_GAIN = 0
SM_EVCW = 72
SM_EVPS = 96
SM_FCW = 104
SM_FCB = 368
SM_BF = 456
SM_N = 458

NDMASEM = 12
SEM_WRAP = 30000


class Op:
    __slots__ = ("eng", "fn", "reads", "writes", "dma", "idx", "deps", "needs_inc",
                 "sem", "val", "waits", "prewait")

    def __init__(self, eng, fn, reads, writes, dma):
        self.eng = eng
        self.fn = fn
        self.reads = reads
        self.writes = writes
        self.dma = dma
        self.deps = ()
        self.needs_inc = False
        self.sem = None
        self.val = 0
        self.waits = []
        self.prewait = None


class Prog:
    def __init__(self, nc):
        self.nc = nc
        self.ops = []

    def op(self, eng, fn, reads=(), writes=()):
        o = Op(eng, fn, tuple(reads), tuple(writes), False)
        self.ops.append(o)
        return o

    def dma(self, eng, out, in_, reads=(), writes=(), **kw):
        o = Op(eng, lambda e: e.dma_start(out=out, in_=in_, **kw), tuple(reads), tuple(writes), True)
        self.ops.append(o)
        return o

    def finalize(self, final_wait_ops=()):
        nc = self.nc
        ops = self.ops
        last_w = {}
        readers = {}
        for i, o in enumerate(ops):
            o.idx = i
            deps = set()
            for r in o.reads:
                w = last_w.get(r)
                if w is not None:
                    deps.add(w)
            for r in o.writes:
                w = last_w.get(r)
                if w is not None:
                    deps.add(w)
                rl = readers.get(r)
                if rl:
                    deps.update(rl)
            deps.discard(i)
            for r in o.reads:
                readers.setdefault(r, []).append(i)
            for r in o.writes:
                last_w[r] = i
                readers[r] = []
            if o.eng == "tensor" and not o.dma:
                o.deps = [d for d in deps if ops[d].eng != "tensor" or ops[d].dma]
            else:
                o.deps = list(deps)
            for d in o.deps:
                ops[d].needs_inc = True
        for o in final_wait_ops:
            o.needs_inc = True
        stack = contextlib.ExitStack()
        sem_objs = {}
        eng_cnt = {}
        dma_rr = {}
        dma_cnt = {}
        for o in ops:
            if o.dma:
                k = dma_rr.get(o.eng, 0)
                dma_rr[o.eng] = k + 1
                slot = k % NDMASEM
                c = dma_cnt.get((o.eng, slot), 0)
                name = f"d_{o.eng}_{slot}"
                if c > 0:
                    o.prewait = (name, c * 16)
                c += 1
                dma_cnt[(o.eng, slot)] = c
                o.sem = name
                o.val = c * 16
            elif o.needs_inc:
                gen, c = eng_cnt.get(o.eng, (0, 0))
                if c >= SEM_WRAP:
                    gen, c = gen + 1, 0
                c += 1
                eng_cnt[o.eng] = (gen, c)
                o.sem = f"e_{o.eng}_{gen}"
                o.val = c
        seen = {}
        for o in ops:
            s = seen.setdefault(o.eng, {})
            need = {}
            if o.prewait is not None:
                need[o.prewait[0]] = o.prewait[1]
            for d in o.deps:
                p = ops[d]
                if need.get(p.sem, 0) < p.val:
                    need[p.sem] = p.val
            for name, v in need.items():
                if s.get(name, 0) < v:
                    s[name] = v
                    o.waits.append((name, v))
        per_eng = {}
        for o in ops:
            per_eng.setdefault(o.eng, []).append(o)
        finals = [(o.sem, o.val, o.eng) for o in final_wait_ops]
        names = sorted({w[0] for o in ops for w in o.waits} | {o.sem for o in ops if o.sem})
        for name in names:
            sem_objs[name] = stack.enter_context(nc.semaphore(name))
        with stack:
            with nc.Block() as block:
                for eng, lst in per_eng.items():
                    def body(e, lst=lst, eng=eng):
                        for o in lst:
                            for name, v in o.waits:
                                e.wait_ge(sem_objs[name], v)
                            ins = o.fn(e)
                            if o.dma:
                                ins.then_inc(sem_objs[o.sem], 16)
                            elif o.needs_inc:
                                ins.then_inc(sem_objs[o.sem], 1)
                        for name, v, feng in finals:
                            if feng == eng:
                                e.wait_ge(sem_objs[name], v)
                    getattr(block, eng)(body)
        return len(ops)


def _t5_bucket(dist):
    dist = np.maximum(dist, 0)
    exact = 16
    d_f = np.maximum(dist, 1).astype(np.float32)
    log_b = exact + (np.log(d_f / np.float32(exact)) / np.float32(np.log(128 / exact))
                     * np.float32(32 - exact)).astype(np.int32)
    log_b = np.minimum(log_b, 31)
    return np.where(dist < exact, dist, log_b)


def _host_consts(rel_bias):
    c = {}
    c["ident"] = np.eye(128, dtype=np.float32)
    sidx = np.arange(128)[:, None]
    tidx = np.arange(128)[None, :]
    c["tri"] = np.where(sidx <= tidx, 0.0, NEG).astype(np.float32)
    t2 = np.arange(640)[None, :]
    dist = t2 - sidx
    bk = _t5_bucket(dist)
    tc = np.empty((8, 128, 640), np.float32)
    for h in range(8):
        tc[h] = np.where(dist >= 0, rel_bias[bk, h], np.float32(NEG))
    c["tcorr"] = tc
    c["b31"] = np.ascontiguousarray(np.broadcast_to(rel_bias[31][None, :], (128, 8))).astype(np.float32)
    ind = np.zeros((16, S), np.float32)
    for n in range(16):
        ind[n, n * 256:(n + 1) * 256] = 1.0
    c["ind"] = ind
    tq = np.arange(32)[:, None]
    nb = np.arange(16)[None, :]
    cur = tq // 2
    valid = (nb < cur).astype(np.float32)
    own = (nb == cur).astype(np.float32)
    cm = np.where(nb < cur, 0.0, -1e30).astype(np.float32)
    c["valid"] = np.ascontiguousarray(np.broadcast_to(valid.reshape(1, 512), (128, 512)))
    c["own"] = np.ascontiguousarray(np.broadcast_to(own.reshape(1, 512), (128, 512)))
    c["cmask"] = np.ascontiguousarray(np.broadcast_to(cm.reshape(1, 512), (128, 512)))
    pc = np.ones((4, 16), np.float32)
    for g, w in enumerate(POOL_W):
        for t in range(16):
            pc[g, t] = w / min(t + 1, w)
    c["poolcorr"] = np.ascontiguousarray(np.broadcast_to(pc.reshape(1, 64), (128, 64)))
    return c


def _host_smalls(inp):
    sm = np.zeros((128, SM_N), np.float32)

    def colmajor(v):
        return np.ascontiguousarray(v.reshape(-1, 128).T)

    for n in range(4):
        sm[:, SM_GAIN + n * 8: SM_GAIN + n * 8 + 8] = colmajor(inp["mix_norm_g"][n])
        sm[:, SM_GAIN + (4 + n) * 8: SM_GAIN + (4 + n) * 8 + 8] = colmajor(inp["ffn_norm_g"][n])
    sm[:, SM_GAIN + 64: SM_GAIN + 72] = colmajor(inp["final_norm_g"])
    for e in range(2):
        for i in range(3):
            sm[:, SM_EVCW + e * 12 + i * 4: SM_EVCW + e * 12 + i * 4 + 4] = colmajor(inp["ev_conv_w"][e, i])
        sm[:, SM_EVPS + e * 4: SM_EVPS + e * 4 + 4] = colmajor(inp["ev_pool_scale"][e])
    for l in range(4):
        for i in range(3):
            sm[:, SM_FCW + l * 66 + i * 22: SM_FCW + l * 66 + i * 22 + 22] = colmajor(inp["ffn_conv_w"][l, i])
        sm[:, SM_FCB + l * 22: SM_FCB + l * 22 + 22] = colmajor(inp["ffn_conv_b"][l])
    for o in range(2):
        sm[0:8, SM_BF + o] = inp["od_b_f"][o]
    return sm


def build_program(debug=False, stop_after=None):
    nc = bass.Bass("TRN2", target_bir_lowering=False)
    dk = "ExternalOutput" if debug else "Internal"

    def din(name, shape, dt=F32):
        return nc.dram_tensor(name, list(shape), dt, kind="ExternalInput").ap()

    x_d = din("x", [S, D])
    ev_w_in = din("ev_w_in", [2, D, 2048])
    ev_pool_w = din("ev_pool_w", [2, 4, 128, 128])
    ev_w_out = din("ev_w_out", [2, D, D])
    od_w_in = din("od_w_in", [2, D, 3080])
    od_w_out = din("od_w_out", [2, D, D])
    ffn_w_in = din("ffn_w_in", [4, D, 2 * DFF])
    ffn_w_out = din("ffn_w_out", [4, DFF, D])
    smalls_d = din("smalls", [128, SM_N])
    ident_d = din("ident", [128, 128])
    tri_d = din("tri", [128, 128])
    tcorr_d = din("tcorr", [8, 128, 640])
    b31_d = din("b31", [128, 8])
    ind_d = din("ind", [16, S])
    valid_d = din("valid", [128, 512])
    own_d = din("own", [128, 512])
    cmask_d = din("cmask", [128, 512])
    poolcorr_d = din("poolcorr", [128, 64])
    out_d = nc.dram_tensor("out", [S, D], F32, kind="ExternalOutput").ap()

    hT = nc.dram_tensor("hT", [D, S], F32, kind=dk).ap()
    hnT = nc.dram_tensor("hnT", [D, S], BF16, kind=dk).ap()
    qT = nc.dram_tensor("qT", [D, S], BF16, kind=dk).ap()
    kT = nc.dram_tensor("kT", [D, S], BF16, kind=dk).ap()
    vv = nc.dram_tensor("vv", [S, D], BF16, kind=dk).ap()
    yT = nc.dram_tensor("yT", [D, S], BF16, kind=dk).ap()
    Fgd = nc.dram_tensor("Fgd", [8, S], BF16, kind=dk).ap()
    nFd = nc.dram_tensor("nFd", [8, S], F32, kind=dk).ap()

    st = contextlib.ExitStack()
    with st:
        def sb(name, shape, dt=F32):
            return st.enter_context(nc.sbuf_tensor("s_" + name, list(shape), dt))

        G = [sb("G0", [128, GCOLS], BF16), sb("G1", [128, GCOLS], BF16)]
        BH = [sb("BH0", [128, 2048]), sb("BH1", [128, 2048])]
        BN = [sb("BN0", [128, 2048], BF16), sb("BN1", [128, 2048], BF16)]
        YT = [sb(f"YT{i}", [128, 2048], BF16) for i in range(4)]
        SQ = sb("SQ", [128, 2048], BF16)
        RS = [sb("RS0", [128, 256]), sb("RS1", [128, 256])]
        E = [sb(f"E{i}", [128, 272]) for i in range(8)]
        ACTT = [sb("ACTT0", [128, 256], BF16), sb("ACTT1", [128, 256], BF16)]
        PB = [sb("PB0", [128, 256], BF16), sb("PB1", [128, 256], BF16)]
        PT = [sb(f"PT{i}", [128, 512], BF16) for i in range(4)]
        RCP = [sb("RCP0", [128, 512]), sb("RCP1", [128, 512])]
        YS = [sb("YS0", [128, 512], BF16), sb("YS1", [128, 512], BF16)]
        ident = sb("ident", [128, 128])
        identb = sb("identb", [128, 128], BF16)
        onesb = sb("onesb", [128, 128], BF16)
        tri = sb("tri", [128, 128])
        smalls = sb("smalls", [128, SM_N])
        TC = [sb("TC0", [128, 640]), sb("TC1", [128, 640])]
        CM = sb("CM", [128, 512])
        VALID = sb("VALID", [128, 512])
        OWN = sb("OWN", [128, 512])
        B31 = sb("B31", [128, 8])
        PCORR = sb("PCORR", [128, 64])
        GS = sb("GS", [128, 512])
        GT = sb("GT", [128, 512])
        TH = sb("TH", [128, 256])
        MV = sb("MV", [128, 32 * 80], BF16)
        KM = sb("KM", [128, 16])
        KMS = sb("KMS", [128, 16])
        KMB = sb("KMB", [128, 32], BF16)
        FCOL = sb("FCOL", [128, 256])
        SP = [sb(f"SP{i}", [8, 256]) for i in range(2)]
        NF = [sb(f"NF{i}", [8, 256]) for i in range(2)]
        FG = [sb(f"FGt{i}", [8, 256], BF16) for i in range(2)]
        NBF = sb("NBF", [8, 2])
        HALO_A = sb("HALO_A", [128, 8])
        HALO_B = sb("HALO_B", [128, 64])
        HALO_F = sb("HALO_F", [128, 44])
        FENCE = sb("FENCE", [128, 8])
        PS = [st.enter_context(nc.psum_tensor(f"B{i}", [128, 512], F32)) for i in range(7)]
        PSB = st.enter_context(nc.psum_tensor("B7", [128, 1024], BF16))
        PS.append(PSB.bitcast(F32))

        P = Prog(nc)
        LD = "sync"
        STQ = "gpsimd"

        def psk(b, h=None):
            return [("B", b)]

        def gkeys(gi, c0, c1):
            return [("G", gi, b) for b in range(c0 // 256, (c1 - 1) // 256 + 1)]

        def bhk(buf, ks=range(8)):
            return [("BH", buf, k) for k in ks]

        def sm(col, n=1, rows=128):
            return smalls[0:rows, col:col + n]

        def v3(ap, n):
            return ap.rearrange("p (k n) -> p k n", n=n)

        def hT_tile(tt):
            return hT[:, tt * TT:(tt + 1) * TT].rearrange("(k p) t -> p k t", p=128)

        def fm_tile(dr, tt, n=TT):
            return dr[:, tt * n:(tt + 1) * n].rearrange("(k p) t -> p k t", p=128)

        ALLT = list(range(NT))

        EPS = sb("EPS", [128, 1])
        ONE = sb("ONE", [128, 1])
        ZERO = sb("ZERO", [128, 1])
        P.dma(LD, smalls[:], smalls_d, writes=["smalls"])
        P.dma(LD, ident[:], ident_d, writes=["ident"])
        P.dma(STQ, identb[:], ident_d, writes=["identb"])
        P.dma(LD, tri[:], tri_d, writes=["tri"])
        P.dma(LD, CM[:], cmask_d, writes=["CM"])
        P.dma(LD, VALID[:], valid_d, writes=["VALID"])
        P.dma(LD, OWN[:], own_d, writes=["OWN"])
        P.dma(LD, B31[:], b31_d, writes=["B31"])
        P.dma(LD, PCORR[:], poolcorr_d, writes=["PCORR"])
        P.op("vector", lambda e: e.memset(onesb[:], 1.0), writes=["onesb"])
        P.op("vector", lambda e: e.memset(FENCE[:], 0.0), writes=["FENCE"])
        P.op("vector", lambda e: e.memset(EPS[:], 1e-6), writes=["EPS"])
        P.op("vector", lambda e: e.memset(ONE[:], 1.0), writes=["ONE"])
        P.op("vector", lambda e: e.memset(ZERO[:], 0.0), writes=["ZERO"])
        for o in range(2):
            P.op("vector", lambda e, o=o: e.tensor_scalar(out=NBF[:, o:o + 1], in0=sm(SM_BF + o, 1, 8),
                                                          scalar1=-1.0, scalar2=None, op0=ALU.mult),
                 reads=["smalls"], writes=[("NBF", o)])

        def load_w(gi, col0, dram2d, nk, stride, ncols):
            for k in range(nk):
                c0 = col0 + k * stride
                P.dma(STQ, G[gi][:, c0:c0 + ncols], dram2d[k * 128:(k + 1) * 128, :],
                      writes=gkeys(gi, c0, c0 + ncols))

        def phase_in():
            def tile(t):
                hp = t % 2
                xin = BH[0][:, hp * 1024:hp * 1024 + 1024]
                xout = BH[1][:, hp * 1024:hp * 1024 + 1024]
                kin = bhk(0, range(4 * hp, 4 * hp + 4))
                kout = bhk(1, range(4 * hp, 4 * hp + 4))
                P.dma(LD, xin, x_d[t * 128:(t + 1) * 128, :], writes=kin)
                for k in range(8):
                    b = hp * 2 + k // 4
                    c = (k % 4) * 128
                    P.op("tensor", lambda e, b=b, c=c, k=k: e.transpose(
                        out=PS[b][:, c:c + 128], in_=xin[:, k * 128:(k + 1) * 128], identity=ident[:]),
                        reads=kin + ["ident"], writes=psk(b))
                for k2 in range(2):
                    b = hp * 2 + k2
                    if k2 == 0:
                        P.op("vector", lambda e, b=b, k2=k2: e.tensor_copy(
                            out=xout[:, k2 * 512:(k2 + 1) * 512], in_=PS[b][:, :]),
                            reads=psk(b), writes=kout[2 * k2:2 * k2 + 2])
                    else:
                        P.op("scalar", lambda e, b=b, k2=k2: e.activation(
                            out=xout[:, k2 * 512:(k2 + 1) * 512], in_=PS[b][:, :], func=AF.Copy),
                            reads=psk(b), writes=kout[2 * k2:2 * k2 + 2])
                P.dma(STQ, hT[:, t * 128:(t + 1) * 128].rearrange("(k p) t -> p k t", p=128),
                      v3(xout, 128), reads=kout, writes=[("hT", t // 2)])
            for t in range(32):
                tile(t)

        def norm_stats(src, n, par, rs, skeys):
            b = 5 + par
            P.op("scalar", lambda e: e.activation(out=SQ[:, 0:8 * n], in_=src, func=AF.Square),
                 reads=skeys, writes=["SQ"])
            for k in range(8):
                P.op("tensor", lambda e, k=k: e.matmul(PS[b][:, 0:n], lhsT=onesb[:],
                                                       rhs=SQ[:, k * n:(k + 1) * n], start=(k == 0), stop=(k == 7)),
                     reads=["SQ", "onesb"], writes=psk(b, par))
            P.op("scalar", lambda e: e.activation(out=rs, in_=PS[b][:, 0:n], func=AF.Sqrt,
                                                  bias=EPS[:, 0:1], scale=1.0 / D),
                 reads=psk(b, par) + ["EPS"], writes=[("RS", par)])
            P.op("vector", lambda e: e.reciprocal(out=rs, in_=rs), reads=[("RS", par)], writes=[("RS", par)])

        def phase_norm(gidx):
            def tile(tt):
                par = tt % 2
                bh = BH[par]
                P.dma(LD, v3(bh[:], TT), hT_tile(tt), reads=[("hT", tt)], writes=bhk(par))
                norm_stats(bh[:], TT, par, RS[par][:], bhk(par))
                for k in range(8):
                    P.op("vector", lambda e, k=k: e.scalar_tensor_tensor(
                        out=BN[par][:, k * TT:(k + 1) * TT], in0=bh[:, k * TT:(k + 1) * TT],
                        scalar=sm(SM_GAIN + gidx * 8 + k), in1=RS[par][:], op0=ALU.mult, op1=ALU.mult),
                        reads=[("BH", par, k), ("RS", par), "smalls"], writes=[("BN", par, k)])
                P.dma(STQ, fm_tile(hnT, tt), v3(BN[par][:], TT), reads=[("BN", par, k) for k in range(8)],
                      writes=[("hnT", tt)])
            for tt in range(NT):
                tile(tt)

        def bnk(par):
            return [("BN", par, k) for k in range(8)]

        def outproj_residual(tt, src, src_keys, gi, wcol, bank=6):
            par = tt % 2
            bh = BH[par]
            P.dma(LD, v3(bh[:], TT), hT_tile(tt), reads=[("hT", tt)], writes=bhk(par))
            for j in range(8):
                hf = j % 2
                bank = 6 + hf
                for k in range(8):
                    c0 = wcol + k * 1024 + j * 128
                    P.op("tensor", lambda e, k=k, c0=c0, bank=bank: e.matmul(
                        PS[bank][:, 0:256], lhsT=G[gi][:, c0:c0 + 128],
                        rhs=src[:, k * TT:(k + 1) * TT], start=(k == 0), stop=(k == 7)),
                        reads=list(src_keys) + gkeys(gi, c0, c0 + 128), writes=psk(bank, hf))
                P.op("vector", lambda e, j=j, bank=bank: e.tensor_tensor(
                    out=bh[:, j * TT:(j + 1) * TT], in0=bh[:, j * TT:(j + 1) * TT],
                    in1=PS[bank][:, 0:256], op=ALU.add),
                    reads=[("BH", par, j)] + psk(bank, hf), writes=[("BH", par, j)])
            P.dma(STQ, hT_tile(tt), v3(bh[:], TT), reads=bhk(par), writes=[("hT", tt)])

        EV_WOUT = 16384
        EV_WPOOL = 24576

        def even_loadw(e_i, gi):
            load_w(gi, 0, ev_w_in[e_i], 8, 2048, 2048)
            load_w(gi, EV_WOUT, ev_w_out[e_i], 8, 1024, 1024)
            P.dma(STQ, G[gi][:, EV_WPOOL:EV_WPOOL + 512].rearrange("p (g d) -> p g d", d=128),
                  ev_pool_w[e_i].rearrange("g c d -> c g d"), writes=gkeys(gi, EV_WPOOL, EV_WPOOL + 512))

        def even_compute(e_i, gi):
            P.op("vector", lambda e: e.memset(HALO_A[:], 0.0), writes=["HALO_A"])
            P.op("vector", lambda e: e.memset(HALO_B[:], 0.0), writes=["HALO_B"])
            Wg = G[gi]
            cw = SM_EVCW + e_i * 12

            def proj(bank, hf, fcol, par):
                for k in range(8):
                    c0 = k * 2048 + fcol
                    P.op("tensor", lambda e, k=k, c0=c0: e.matmul(
                        PS[bank][:, 0:256], lhsT=Wg[:, c0:c0 + 128],
                        rhs=BN[par][:, k * TT:(k + 1) * TT], start=(k == 0), stop=(k == 7)),
                        reads=bnk(par) + gkeys(gi, c0, c0 + 128), writes=psk(bank, hf))

            def mixA(tt, i):
                par = tt % 2
                yt = YT[par]
                hf = i % 2
                b0 = 3 * hf
                proj(b0, hf, i * 128, par)
                proj(b0 + 1, hf, 512 + i * 128, par)
                proj(b0 + 2, hf, 1024 + i * 128, par)
                egc, cv, ea = E[hf], E[2 + hf], E[4 + hf]
                kg, kc, ka = ("E", hf), ("E", 2 + hf), ("E", 4 + hf)
                P.op("scalar", lambda e: e.activation(out=egc[:, 0:256], in_=PS[b0 + 1][:, 0:256], func=AF.Copy),
                     reads=psk(b0 + 1), writes=[kg])
                P.op("scalar", lambda e: e.activation(out=cv[:, 0:2], in_=HALO_A[:, 2 * i:2 * i + 2], func=AF.Copy),
                     reads=["HALO_A"], writes=[kc])
                P.op("vector", lambda e: e.tensor_tensor(
                    out=cv[:, 2:258], in0=egc[:, 0:256], in1=PS[b0 + 2][:, 0:256], op=ALU.mult),
                    reads=[kg] + psk(b0 + 2), writes=[kc])
                P.op("scalar", lambda e: e.activation(out=HALO_A[:, 2 * i:2 * i + 2], in_=cv[:, 256:258], func=AF.Copy),
                     reads=[kc], writes=["HALO_A"])
                P.op("vector", lambda e: e.tensor_scalar(
                    out=ea[:, 0:256], in0=cv[:, 2:258], scalar1=sm(cw + 8 + i), scalar2=None, op0=ALU.mult),
                    reads=[kc, "smalls"], writes=[ka])
                for ci, off in ((1, 1), (0, 0)):
                    P.op("vector", lambda e, ci=ci, off=off: e.scalar_tensor_tensor(
                        out=ea[:, 0:256], in0=cv[:, off:off + 256], scalar=sm(cw + ci * 4 + i), in1=ea[:, 0:256],
                        op0=ALU.mult, op1=ALU.add),
                        reads=[kc, ka, "smalls"], writes=[ka])
                P.op("vector", lambda e: e.tensor_tensor(
                    out=yt[:, i * TT:(i + 1) * TT], in0=ea[:, 0:256], in1=PS[b0][:, 0:256], op=ALU.mult),
                    reads=[ka] + psk(b0), writes=[("YT", par, i)])

            def mixB(tt, g):
                par = tt % 2
                yt = YT[par]
                hf = g % 2
                w = POOL_W[g]
                b0 = 3 * hf
                proj(b0, hf, 1536 + g * 128, par)
                up = E[6 + hf]
                ku = ("E", 6 + hf)
                P.op("scalar", lambda e: e.activation(out=up[:, 0:16], in_=HALO_B[:, 16 * g:16 * g + 16], func=AF.Copy),
                     reads=["HALO_B"], writes=[ku])
                P.op("scalar", lambda e: e.activation(out=up[:, 16:272], in_=PS[b0][:, 0:256], func=AF.Copy),
                     reads=psk(b0), writes=[ku])
                P.op("scalar", lambda e: e.activation(out=HALO_B[:, 16 * g:16 * g + 16], in_=up[:, 256:272], func=AF.Copy),
                     reads=[ku], writes=["HALO_B"])
                cur, curk = up, ku
                dst = [(E[hf], ("E", hf)), (E[2 + hf], ("E", 2 + hf))]
                step = 1
                di = 0
                while step < w:
                    lo = 2 * step - 1
                    o_t, o_k = dst[di % 2]
                    P.op("vector", lambda e, cur=cur, o_t=o_t, lo=lo, step=step: e.tensor_tensor(
                        out=o_t[:, lo:272], in0=cur[:, lo:272], in1=cur[:, lo - step:272 - step], op=ALU.add),
                        reads=[curk], writes=[o_k])
                    cur, curk = o_t, o_k
                    step *= 2
                    di += 1
                if tt == 0:
                    P.op("vector", lambda e, cur=cur: e.tensor_tensor(
                        out=cur[:, 16:32], in0=cur[:, 16:32], in1=PCORR[:, 16 * g:16 * g + 16], op=ALU.mult),
                        reads=[curk, "PCORR"], writes=[curk])
                P.op("vector", lambda e, cur=cur: e.scalar_tensor_tensor(
                    out=PB[hf][:, :], in0=cur[:, 16:272], scalar=1.0 / w, in1=up[:, 16:272],
                    op0=ALU.mult, op1=ALU.subtract),
                    reads=[curk, ku], writes=[("PB", hf)])
                c0 = EV_WPOOL + g * 128
                P.op("tensor", lambda e: e.matmul(
                    PS[b0 + 1][:, 0:256], lhsT=Wg[:, c0:c0 + 128], rhs=PB[hf][:, :], start=True, stop=True),
                    reads=[("PB", hf)] + gkeys(gi, c0, c0 + 128), writes=psk(b0 + 1))
                P.op("scalar", lambda e: e.activation(
                    out=yt[:, (4 + g) * TT:(5 + g) * TT], in_=PS[b0 + 1][:, 0:256], func=AF.Copy,
                    scale=sm(SM_EVPS + e_i * 4 + g)),
                    reads=psk(b0 + 1) + ["smalls"], writes=[("YT", par, 4 + g)])

            for tt in range(NT):
                par = tt % 2
                P.dma(LD, v3(BN[par][:], TT), fm_tile(hnT, tt), reads=[("hnT", tt)], writes=bnk(par))
                for i in range(4):
                    mixA(tt, i)
                for g in range(4):
                    mixB(tt, g)
                outproj_residual(tt, YT[par], [("YT", par, i) for i in range(8)], gi, EV_WOUT)

        def ffn_loadw(l, grp, gi):
            c_first, ncg = FFN_GROUPS[grp]
            nu = ncg * 128
            KST = 2 * nu
            WOUT = 8 * KST
            for k in range(8):
                c0 = k * KST
                P.dma(STQ, G[gi][:, c0:c0 + nu], ffn_w_in[l, k * 128:(k + 1) * 128, c_first * 128:c_first * 128 + nu],
                      writes=gkeys(gi, c0, c0 + nu))
                P.dma(STQ, G[gi][:, c0 + nu:c0 + 2 * nu],
                      ffn_w_in[l, k * 128:(k + 1) * 128, DFF + c_first * 128:DFF + c_first * 128 + nu],
                      writes=gkeys(gi, c0 + nu, c0 + 2 * nu))
            for cl in range(ncg):
                c0 = WOUT + cl * 1024
                P.dma(STQ, G[gi][:, c0:c0 + 1024], ffn_w_out[l, (c_first + cl) * 128:(c_first + cl + 1) * 128, :],
                      writes=gkeys(gi, c0, c0 + 1024))

        def ffn_compute(l, grp, gi):
            c_first, ncg = FFN_GROUPS[grp]
            nu = ncg * 128
            KST = 2 * nu
            WOUT = 8 * KST
            P.op("vector", lambda e: e.memset(HALO_F[:], 0.0), writes=["HALO_F"])
            Wg = G[gi]
            cw = SM_FCW + l * 66
            cb = SM_FCB + l * 22

            def tile(tt):
                par = tt % 2
                bh = BH[par]
                P.dma(LD, v3(BN[par][:], TT), fm_tile(hnT, tt), reads=[("hnT", tt)], writes=bnk(par))
                P.dma(LD, v3(bh[:], TT), hT_tile(tt), reads=[("hT", tt)], writes=bhk(par))

                def proj(cl):
                    hf = cl % 2
                    for part, bank in ((0, hf), (1, 2 + hf)):
                        for k in range(8):
                            c0 = k * KST + part * nu + cl * 128
                            P.op("tensor", lambda e, k=k, c0=c0, bank=bank: e.matmul(
                                PS[bank][:, 0:256], lhsT=Wg[:, c0:c0 + 128],
                                rhs=BN[par][:, k * TT:(k + 1) * TT], start=(k == 0), stop=(k == 7)),
                                reads=bnk(par) + gkeys(gi, c0, c0 + 128), writes=psk(bank, hf))

                def elem(cl):
                    hf = cl % 2
                    cg = c_first + cl
                    up, ea, es = E[hf], E[2 + hf], E[4 + hf]
                    ku, ka, ks = ("E", hf), ("E", 2 + hf), ("E", 4 + hf)
                    P.op("scalar", lambda e: e.activation(out=up[:, 0:2], in_=HALO_F[:, 2 * cg:2 * cg + 2], func=AF.Copy),
                         reads=["HALO_F"], writes=[ku])
                    P.op("scalar", lambda e: e.activation(out=up[:, 2:258], in_=PS[hf][:, 0:256], func=AF.Copy),
                         reads=psk(hf), writes=[ku])
                    P.op("scalar", lambda e: e.activation(out=HALO_F[:, 2 * cg:2 * cg + 2], in_=up[:, 256:258], func=AF.Copy),
                         reads=[ku], writes=["HALO_F"])
                    P.op("vector", lambda e: e.tensor_scalar(
                        out=ea[:, 0:256], in0=up[:, 2:258], scalar1=sm(cw + 44 + cg), scalar2=sm(cb + cg),
                        op0=ALU.mult, op1=ALU.add),
                        reads=[ku, "smalls"], writes=[ka])
                    for ci, off in ((1, 1), (0, 0)):
                        P.op("vector", lambda e, ci=ci, off=off: e.scalar_tensor_tensor(
                            out=ea[:, 0:256], in0=up[:, off:off + 256], scalar=sm(cw + ci * 22 + cg), in1=ea[:, 0:256],
                            op0=ALU.mult, op1=ALU.add),
                            reads=[ku, ka, "smalls"], writes=[ka])
                    P.op("scalar", lambda e: e.activation(out=es[:, 0:256], in_=ea[:, 0:256], func=AF.Silu),
                         reads=[ka], writes=[ks])
                    P.op("vector", lambda e: e.tensor_tensor(
                        out=ACTT[hf][:, :], in0=es[:, 0:256], in1=PS[2 + hf][:, 0:256], op=ALU.mult),
                        reads=[ks] + psk(2 + hf), writes=[("ACTT", hf)])

                def outp(cl):
                    hf = cl % 2
                    for j in range(8):
                        c0 = WOUT + cl * 1024 + j * 128
                        bank = 4 + j // 2
                        P.op("tensor", lambda e, c0=c0, j=j, bank=bank: e.matmul(
                            PS[bank][:, (j % 2) * 256:(j % 2 + 1) * 256], lhsT=Wg[:, c0:c0 + 128], rhs=ACTT[hf][:, :],
                            start=(cl == 0 and j % 2 == 0), stop=(cl == ncg - 1 and j % 2 == 1)),
                            reads=[("ACTT", hf)] + gkeys(gi, c0, c0 + 128), writes=psk(bank, j % 2))

                proj(0)
                for cl in range(ncg):
                    if cl + 1 < ncg:
                        proj(cl + 1)
                    elem(cl)
                    outp(cl)
                for j in range(8):
                    bank = 4 + j // 2
                    P.op("vector", lambda e, j=j, bank=bank: e.tensor_tensor(
                        out=bh[:, j * TT:(j + 1) * TT], in0=bh[:, j * TT:(j + 1) * TT],
                        in1=PS[bank][:, (j % 2) * 256:(j % 2 + 1) * 256], op=ALU.add),
                        reads=[("BH", par, j)] + psk(bank, j % 2), writes=[("BH", par, j)])
                P.dma(STQ, hT_tile(tt), v3(bh[:], TT), reads=bhk(par), writes=[("hT", tt)])

            for tt in range(NT):
                tile(tt)

        KST_O = 3080

        def oddproj_loadw(o_i, gi):
            load_w(gi, 0, od_w_in[o_i], 8, KST_O, 3080)

        def phase_odd_proj(o_i, gi):
            Wg = G[gi]
            stg = [0]

            def next_stage():
                i = stg[0] % 4
                stg[0] += 1
                return i

            def tile(tt):
                par = tt % 2
                P.dma(LD, v3(BN[par][:], TT), fm_tile(hnT, tt), reads=[("hnT", tt)], writes=bnk(par))
                for which, dst, scale in ((0, qT, 0.125), (1, kT, 1.0)):
                    si = next_stage()
                    for c in range(8):
                        bank = c % 2
                        hf = 0
                        for k in range(8):
                            c0 = k * KST_O + which * 1024 + c * 128
                            P.op("tensor", lambda e, k=k, c0=c0, bank=bank, hf=hf: e.matmul(
                                PS[bank][:, hf * 256:(hf + 1) * 256], lhsT=Wg[:, c0:c0 + 128],
                                rhs=BN[par][:, k * TT:(k + 1) * TT], start=(k == 0), stop=(k == 7)),
                                reads=bnk(par) + gkeys(gi, c0, c0 + 128), writes=psk(bank, hf))
                        if c % 2 == 0:
                            P.op("scalar", lambda e, c=c, bank=bank, hf=hf, si=si, scale=scale: e.activation(
                                out=YT[si][:, c * TT:(c + 1) * TT], in_=PS[bank][:, hf * 256:(hf + 1) * 256],
                                func=AF.Copy, scale=scale),
                                reads=psk(bank, hf), writes=[("YT", si, c)])
                        else:
                            P.op("vector", lambda e, c=c, bank=bank, hf=hf, si=si, scale=scale: e.tensor_scalar(
                                out=YT[si][:, c * TT:(c + 1) * TT], in0=PS[bank][:, hf * 256:(hf + 1) * 256],
                                scalar1=scale, scalar2=None, op0=ALU.mult),
                                reads=psk(bank, hf), writes=[("YT", si, c)])
                    P.dma(STQ, fm_tile(dst, tt), v3(YT[si][:], TT), reads=[("YT", si, c) for c in range(8)],
                          writes=[("qT" if which == 0 else "kT", tt)])
                si = next_stage()
                for sub in range(2):
                    for half in range(2):
                        bank = 2 + half
                        for k in range(8):
                            c0 = k * KST_O + 2048 + half * 512
                            P.op("tensor", lambda e, k=k, c0=c0, bank=bank, sub=sub: e.matmul(
                                PS[bank][:, :], lhsT=BN[par][:, k * TT + sub * 128:k * TT + sub * 128 + 128],
                                rhs=Wg[:, c0:c0 + 512], start=(k == 0), stop=(k == 7)),
                                reads=bnk(par) + gkeys(gi, c0, c0 + 512), writes=psk(bank))
                        oc = sub * 1024 + half * 512
                        wk = [("YT", si, (oc // 256) + i) for i in range(2)]
                        if half == 0:
                            P.op("vector", lambda e, bank=bank, oc=oc, si=si: e.tensor_copy(
                                out=YT[si][:, oc:oc + 512], in_=PS[bank][:, :]),
                                reads=psk(bank), writes=wk)
                        else:
                            P.op("scalar", lambda e, bank=bank, oc=oc, si=si: e.activation(
                                out=YT[si][:, oc:oc + 512], in_=PS[bank][:, :], func=AF.Copy),
                                reads=psk(bank), writes=wk)
                P.dma(STQ, vv[tt * TT:(tt + 1) * TT, :].rearrange("(s p) f -> p s f", p=128),
                      YT[si][:].rearrange("p (s f) -> p s f", f=1024),
                      reads=[("YT", si, c) for c in range(8)], writes=[("vv", tt)])
                for k in range(8):
                    c0 = k * KST_O + 3072
                    P.op("tensor", lambda e, k=k, c0=c0: e.matmul(
                        PS[4 + par][0:8, 0:256], lhsT=Wg[:, c0:c0 + 8],
                        rhs=BN[par][:, k * TT:(k + 1) * TT], start=(k == 0), stop=(k == 7)),
                        reads=bnk(par) + gkeys(gi, c0, c0 + 8), writes=psk(4 + par))
                P.op("scalar", lambda e: e.activation(
                    out=SP[par][:, :], in_=PS[4 + par][0:8, 0:256], func=AF.Exp,
                    bias=NBF[:, o_i:o_i + 1], scale=-1.0),
                    reads=psk(4 + par) + [("NBF", o_i)], writes=[("SP", par)])
                P.op("scalar", lambda e: e.activation(
                    out=SP[par][:, :], in_=SP[par][:, :], func=AF.Ln, bias=ONE[0:8, 0:1], scale=1.0),
                    reads=[("SP", par), "ONE"], writes=[("SP", par)])
                init = 0.0 if tt == 0 else NF[1 - par][:, 255:256]
                P.op("vector", lambda e: e.tensor_tensor_scan(
                    out=NF[par][:, :], data0=SP[par][:, :], data1=SP[par][:, :], initial=init,
                    op0=ALU.add, op1=ALU.bypass),
                    reads=[("SP", par), ("NF", 1 - par)], writes=[("NF", par)])
                P.op("vector", lambda e: e.tensor_scalar(
                    out=FG[par][:, :], in0=NF[par][:, :], scalar1=-1.0, scalar2=None, op0=ALU.mult),
                    reads=[("NF", par)], writes=[("FG", par)])
                P.dma(STQ, nFd[:, tt * TT:(tt + 1) * TT], NF[par][:, :], reads=[("NF", par)], writes=[("nFd", tt)])
                P.dma(STQ, Fgd[:, tt * TT:(tt + 1) * TT], FG[par][:, :], reads=[("FG", par)], writes=[("Fgd", tt)])

            for tt in range(NT):
                tile(tt)

        def phase_attention(o_i, gi, nheads=16):
            GA = G[gi]
            allg = gkeys(gi, 0, GCOLS)

            def QA(pr):
                return GA[:, pr * 12288:pr * 12288 + 4096]

            def KA(pr):
                return GA[:, pr * 12288 + 4096:pr * 12288 + 8192]

            def VE(pr):
                return GA[:, pr * 12288 + 8192:pr * 12288 + 12288]

            att_keys = [("att", pr, nm) for pr in range(2) for nm in ("Q", "Qa", "K", "Ka", "V", "Vo")]
            P.op("vector", lambda e: e.memset(FENCE[:, 0:1], 0.0), writes=allg + att_keys + ["FENCE"])
            for pr in range(2):
                P.op("vector", lambda e, pr=pr: e.memset(
                    VE(pr).rearrange("p (t c) -> p t c", c=128)[:, :, 64:128], 1.0),
                    reads=["FENCE"], writes=[("att", pr, "Vo")])
            P.op("vector", lambda e: e.memset(MV[:], 0.0), writes=["MV"])
            P.dma(LD, FCOL[:].rearrange("p (h k) -> p h k", k=32),
                  nFd.rearrange("h (k p) -> p h k", p=128), reads=[("nFd", t) for t in ALLT],
                  writes=["FCOL"], allow_slow_non_contiguous=True)

            ka_mode = [None, None]

            def prep(h):
                pr = h % 2
                fox = h < 8
                hm = h - 8
                P.dma(LD, QA(pr)[0:64, :], qT[h * 64:(h + 1) * 64, :],
                      reads=[("qT", t) for t in ALLT], writes=[("att", pr, "Q")])
                P.dma(LD, KA(pr)[0:64, :], kT[h * 64:(h + 1) * 64, :],
                      reads=[("kT", t) for t in ALLT], writes=[("att", pr, "K")])
                P.dma(LD, VE(pr).rearrange("p (t c) -> p t c", c=128)[:, :, 0:64],
                      vv[:, h * 64:(h + 1) * 64].rearrange("(t p) d -> p t d", p=128),
                      reads=[("vv", t) for t in ALLT], writes=[("att", pr, "V")])
                if fox:
                    if ka_mode[pr] != "fox":
                        P.op("vector", lambda e: e.memset(KA(pr)[64:65, :], 1.0),
                             reads=["FENCE"], writes=[("att", pr, "Ka")])
                        ka_mode[pr] = "fox"
                    P.dma(LD, QA(pr)[64:65, :], Fgd[h:h + 1, :], reads=[("Fgd", t) for t in ALLT],
                          writes=[("att", pr, "Qa")])
                    return
                if ka_mode[pr] != "moba":
                    P.dma(STQ, KA(pr)[64:80, :], ind_d, writes=[("att", pr, "Ka")])
                    ka_mode[pr] = "moba"
                P.dma(LD, TC[hm % 2][:], tcorr_d[hm], writes=[("TC", hm % 2)])
                P.op("vector", lambda e: e.tensor_reduce(
                    out=KM[0:64, :], in_=KA(pr)[0:64, :].rearrange("p (n s) -> p n s", s=256), axis=AX.X, op=ALU.add),
                    reads=[("att", pr, "K")], writes=["KM"])
                P.op("vector", lambda e: e.tensor_scalar(out=KMS[0:64, :], in0=KM[0:64, :], scalar1=1.0 / 256,
                                                         scalar2=None, op0=ALU.mult),
                     reads=["KM"], writes=["KMS"])
                P.op("vector", lambda e: e.tensor_copy(out=KMB[0:64, 0:16], in_=KMS[0:64, :]),
                     reads=["KMS"], writes=[("KMB", 0)])
                P.op("vector", lambda e: e.tensor_tensor(out=KMB[0:64, 16:32], in0=KMS[0:64, :], in1=KMB[0:64, 0:16],
                                                         op=ALU.subtract),
                     reads=["KMS", ("KMB", 0)], writes=[("KMB", 1)])
                for bk in range(2):
                    for t16 in range(16):
                        tq = bk * 16 + t16
                        P.op("tensor", lambda e, tq=tq, bk=bk, t16=t16: e.matmul(
                            PS[5 + bk][:, t16 * 32:(t16 + 1) * 32], lhsT=QA(pr)[0:64, tq * 128:(tq + 1) * 128],
                            rhs=KMB[0:64, 0:32], start=True, stop=True),
                            reads=[("att", pr, "Q"), ("KMB", 0), ("KMB", 1)], writes=psk(5 + bk))
                    P.op("scalar", lambda e, bk=bk: e.activation(out=GS[:, :], in_=PS[5 + bk][:, :], func=AF.Copy),
                         reads=psk(5 + bk), writes=["GS"])
                    gs3 = GS[:, :].rearrange("p (t c) -> p t c", c=32)
                    gtb = GT[:, bk * 256:(bk + 1) * 256].rearrange("p (t c) -> p t c", c=16)
                    P.op("vector", lambda e, gs3=gs3, gtb=gtb: e.tensor_tensor(
                        out=gtb, in0=gs3[:, :, 0:16], in1=gs3[:, :, 16:32], op=ALU.add),
                        reads=["GS"], writes=[("GT", bk)])
                P.op("vector", lambda e: e.tensor_tensor(out=GT[:, :], in0=GT[:, :], in1=CM[:, :], op=ALU.add),
                     reads=[("GT", 0), ("GT", 1), "CM"], writes=[("GT", 0), ("GT", 1)])
                for tq in range(32):
                    P.op("vector", lambda e, tq=tq: e.max(out=TH[:, tq * 8:(tq + 1) * 8], in_=GT[:, tq * 16:(tq + 1) * 16]),
                         reads=[("GT", tq // 16)], writes=["TH"])
                thb = bass.AP(TH, 2, [[256, 128], [8, 32], [0, 16]])
                gt3 = GT[:, :].rearrange("p (t c) -> p t c", c=16)
                gk = [("GT", 0), ("GT", 1)]
                P.op("vector", lambda e: e.tensor_tensor(out=gt3, in0=gt3, in1=thb, op=ALU.is_ge),
                     reads=gk + ["TH"], writes=gk)
                P.op("vector", lambda e: e.tensor_tensor(out=GT[:, :], in0=GT[:, :], in1=VALID[:, :], op=ALU.mult),
                     reads=gk + ["VALID"], writes=gk)
                P.op("vector", lambda e: e.tensor_tensor(out=GT[:, :], in0=GT[:, :], in1=OWN[:, :], op=ALU.add),
                     reads=gk + ["OWN"], writes=gk)
                mv3 = MV[:, :].rearrange("p (t c) -> p t c", c=80)
                P.op("vector", lambda e: e.tensor_scalar(
                    out=mv3[:, :, 64:80], in0=gt3, scalar1=-1.0, scalar2=-NEG, op0=ALU.add, op1=ALU.mult),
                    reads=gk, writes=["MV"])
                for q4 in range(8):
                    hb = q4 % 2
                    for i in range(4):
                        tq = q4 * 4 + i
                        P.op("tensor", lambda e, tq=tq, hb=hb, i=i: e.transpose(
                            out=PSB[0:80, hb * 512 + i * 128:hb * 512 + (i + 1) * 128],
                            in_=MV[:, tq * 80:(tq + 1) * 80], identity=identb[:]),
                            reads=["MV", "identb"], writes=[("B", 7)])
                    P.op("scalar", lambda e, q4=q4, hb=hb: e.activation(
                        out=QA(pr)[64:80, q4 * 512:(q4 + 1) * 512], in_=PSB[64:80, hb * 512:(hb + 1) * 512], func=AF.Copy),
                        reads=[("B", 7)], writes=[("att", pr, "Qa")])

            work = []
            for h in range(nheads):
                for qi in range(8):
                    for kt in range(4 * qi + 4):
                        work.append((h, qi, kt))
            nwork = len(work)
            DEPTH_S = 2

            def s_stage(i):
                h, qi, kt = work[i]
                pr = h % 2
                R = 65 if h < 8 else 80
                c0 = max(0, (kt - 4 * qi) * 128)
                sbk = i % 3
                P.op("tensor", lambda e: e.matmul(
                    PS[sbk][:, c0:512], lhsT=KA(pr)[0:R, kt * 128:(kt + 1) * 128],
                    rhs=QA(pr)[0:R, qi * 512 + c0:(qi + 1) * 512], start=True, stop=True),
                    reads=[("att", pr, "Q"), ("att", pr, "Qa"), ("att", pr, "K"), ("att", pr, "Ka")], writes=psk(sbk))

            def pv_stage(i):
                h, qi, kt = work[i]
                pr = h % 2
                fox = h < 8
                hm = h - 8
                c0 = max(0, (kt - 4 * qi) * 128)
                sbk = i % 3
                last = 4 * qi + 3
                pt = PT[i % 4]
                ob = 3 + (qi % 2)
                if fox:
                    if kt >= 4 * qi:
                        P.op("vector", lambda e: e.tensor_tensor(
                            out=PS[sbk][:, c0:c0 + 128], in0=PS[sbk][:, c0:c0 + 128], in1=tri[:, :], op=ALU.add),
                            reads=psk(sbk) + ["tri"], writes=psk(sbk))
                    bias = FCOL[:, h * 32 + kt:h * 32 + kt + 1]
                    bkk = ["FCOL"]
                else:
                    tcb = TC[hm % 2]
                    if kt >= 4 * qi:
                        P.op("vector", lambda e: e.tensor_tensor(
                            out=PS[sbk][:, c0:512], in0=PS[sbk][:, c0:512], in1=tcb[:, 0:512 - c0], op=ALU.add),
                            reads=psk(sbk) + [("TC", hm % 2)], writes=psk(sbk))
                        bias = ZERO[:, 0:1]
                        bkk = ["ZERO"]
                    elif kt == 4 * qi - 1:
                        P.op("vector", lambda e: e.tensor_tensor(
                            out=PS[sbk][:, 0:512], in0=PS[sbk][:, 0:512], in1=tcb[:, 128:640], op=ALU.add),
                            reads=psk(sbk) + [("TC", hm % 2)], writes=psk(sbk))
                        bias = ZERO[:, 0:1]
                        bkk = ["ZERO"]
                    else:
                        bias = B31[:, hm:hm + 1]
                        bkk = ["B31"]
                P.op("scalar", lambda e: e.activation(out=pt[:, c0:512], in_=PS[sbk][:, c0:512], func=AF.Exp,
                                                      bias=bias, scale=1.0),
                     reads=psk(sbk) + bkk, writes=[("PT", i % 4)])
                P.op("tensor", lambda e: e.matmul(
                    PS[ob][:, c0:512], lhsT=VE(pr)[:, kt * 128:(kt + 1) * 128], rhs=pt[:, c0:512],
                    start=(kt == 0), stop=(kt == last)),
                    reads=[("PT", i % 4), ("att", pr, "V"), ("att", pr, "Vo")], writes=psk(ob))
                if kt == last:
                    rp = (h * 8 + qi) % 2
                    P.op("vector", lambda e: e.reciprocal(out=RCP[rp][64:128, :], in_=PS[ob][64:128, :]),
                         reads=psk(ob), writes=[("RCP", rp)])
                    P.op("vector", lambda e: e.tensor_tensor(out=YS[rp][0:64, :], in0=PS[ob][0:64, :], in1=RCP[rp][64:128, :],
                                                             op=ALU.mult),
                         reads=psk(ob) + [("RCP", rp)], writes=[("YS", rp)])
                    P.dma(STQ, yT[h * 64:(h + 1) * 64, qi * 512:(qi + 1) * 512], YS[rp][0:64, :],
                          reads=[("YS", rp)], writes=[("yT", 2 * qi), ("yT", 2 * qi + 1)])

            prep(0)
            if nheads > 1:
                prep(1)
            for i in range(min(DEPTH_S, nwork)):
                s_stage(i)
            for i in range(nwork):
                if i + DEPTH_S < nwork:
                    s_stage(i + DEPTH_S)
                pv_stage(i)
                h, qi, kt = work[i]
                if qi == 7 and kt == 31 and h + 2 < nheads:
                    prep(h + 2)
            P.op("vector", lambda e: e.memset(FENCE[:, 1:2], 0.0), writes=allg + att_keys + ["FENCE"])

        def oddout_loadw(o_i, gi):
            load_w(gi, 0, od_w_out[o_i], 8, 1024, 1024)

        def oddout_compute(o_i, gi):
            for tt in range(NT):
                par = tt % 2
                P.dma(LD, v3(BN[par][:], TT), fm_tile(yT, tt), reads=[("yT", tt)], writes=bnk(par))
                outproj_residual(tt, BN[par], bnk(par), gi, 0)

        finals = []

        def phase_final():
            gidx = 8

            def tile(t):
                par = t % 2
                hin = BH[0][:, par * 1024:(par + 1) * 1024]
                hout = BH[1][:, par * 1024:(par + 1) * 1024]
                kin = bhk(0, range(4 * par, 4 * par + 4))
                kout = bhk(1, range(4 * par, 4 * par + 4))
                P.dma(LD, v3(hin, 128), hT[:, t * 128:(t + 1) * 128].rearrange("(k p) t -> p k t", p=128),
                      reads=[("hT", t // 2)], writes=kin)
                norm_stats(hin, 128, par, RS[par][:, 0:128], kin)
                P.op("vector", lambda e: e.tensor_tensor(
                    out=v3(hin, 128), in0=v3(hin, 128),
                    in1=bass.AP(RS[par], 0, [[256, 128], [0, 8], [1, 128]]), op=ALU.mult),
                    reads=kin + [("RS", par)], writes=kin)
                for k in range(8):
                    P.op("vector", lambda e, k=k: e.tensor_scalar(
                        out=hin[:, k * 128:(k + 1) * 128], in0=hin[:, k * 128:(k + 1) * 128],
                        scalar1=sm(SM_GAIN + gidx * 8 + k), scalar2=None, op0=ALU.mult),
                        reads=kin + ["smalls"], writes=kin)
                for k in range(8):
                    b = par * 2 + k // 4
                    c = (k % 4) * 128
                    P.op("tensor", lambda e, b=b, c=c, k=k: e.transpose(
                        out=PS[b][:, c:c + 128], in_=hin[:, k * 128:(k + 1) * 128], identity=ident[:]),
                        reads=kin + ["ident"], writes=psk(b))
                for k2 in range(2):
                    b = par * 2 + k2
                    if k2 == 0:
                        P.op("vector", lambda e, b=b, k2=k2: e.tensor_copy(
                            out=hout[:, k2 * 512:(k2 + 1) * 512], in_=PS[b][:, :]),
                            reads=psk(b), writes=kout[2 * k2:2 * k2 + 2])
                    else:
                        P.op("scalar", lambda e, b=b, k2=k2: e.activation(
                            out=hout[:, k2 * 512:(k2 + 1) * 512], in_=PS[b][:, :], func=AF.Copy),
                            reads=psk(b), writes=kout[2 * k2:2 * k2 + 2])
                finals.append(P.dma(STQ, out_d[t * 128:(t + 1) * 128, :], hout, reads=kout, writes=[("out", t)]))

            for t in range(32):
                tile(t)

        items = [("in", None, lambda gi: phase_in())]
        for l in range(4):
            items.append((f"norm_mix{l}", None, lambda gi, l=l: phase_norm(l)))
            if l % 2 == 0:
                items.append((f"even{l}", lambda gi, l=l: even_loadw(l // 2, gi), lambda gi, l=l: even_compute(l // 2, gi)))
            else:
                items.append((f"proj{l}", lambda gi, l=l: oddproj_loadw(l // 2, gi), lambda gi, l=l: phase_odd_proj(l // 2, gi)))
                items.append((f"att{l}", "same", lambda gi, l=l: phase_attention(l // 2, gi)))
                items.append((f"oout{l}", lambda gi, l=l: oddout_loadw(l // 2, gi), lambda gi, l=l: oddout_compute(l // 2, gi)))
            items.append((f"norm_ffn{l}", None, lambda gi, l=l: phase_norm(4 + l)))
            for grp in range(3):
                items.append((f"ffn{l}_{grp}", lambda gi, l=l, grp=grp: ffn_loadw(l, grp, gi),
                              lambda gi, l=l, grp=grp: ffn_compute(l, grp, gi)))
        items.append(("final", None, lambda gi: phase_final()))
        if stop_after is not None:
            names = [it[0] for it in items]
            items = items[:names.index(stop_after) + 1]
        gis = []
        cnt = 0
        for name, lw, cp in items:
            if lw is None:
                gis.append(None)
            elif lw == "same":
                gis.append(gis[-1])
            else:
                gis.append(cnt % 2)
                cnt += 1
        widx = [i for i, it in enumerate(items) if callable(it[1])]
        loaded = set()

        def ensure_loaded(i):
            if i not in loaded:
                loaded.add(i)
                items[i][1](gis[i])

        for i, (name, lw, cp) in enumerate(items):
            if callable(lw):
                ensure_loaded(i)
                nxt = [j for j in widx if j > i]
                if nxt:
                    j = nxt[0]
                    if not (name.startswith("proj")):
                        ensure_loaded(j)
                    else:
                        ensure_loaded(j)
            cp(gis[i])
        if not finals:
            ks = []
            for nm in ("hT", "hnT", "yT", "qT", "kT", "vv", "nFd", "Fgd"):
                ks += [(nm, t) for t in ALLT]
            finals.append(P.dma(STQ, out_d[0:128, 0:128], ident[:], reads=["ident"] + ks))
        nops = P.finalize(final_wait_ops=finals)
    return nc, nops


_WEIGHT_KEYS = ("ev_w_in", "ev_pool_w", "ev_w_out", "od_w_in", "od_w_out", "ffn_w_in", "ffn_w_out")


def make_in_maps(inputs, cores):
    inp = {k: np.ascontiguousarray(np.asarray(v, dtype=np.float32)) for k, v in inputs.items()}
    consts = _host_consts(inp["rel_bias"])
    smalls = _host_smalls(inp)
    shared = {k: inp[k] for k in _WEIGHT_KEYS}
    shared["smalls"] = smalls
    shared.update(consts)
    maps = []
    for b in cores:
        m = dict(shared)
        m["x"] = np.ascontiguousarray(inp["x"][b])
        maps.append(m)
    return maps


def kernel(**inputs):
    nc, _ = build_program()
    maps = make_in_maps(inputs, list(range(8)))
    res = run_bass_kernel_spmd(nc, maps, core_ids=list(range(8)))
    return np.stack([np.asarray(r["out"], dtype=np.float32) for r in res.results], axis=0)
```
